# Optimizing a Trainium2 kernel written in Bass

```python
import jax, jax.numpy as jnp
from jax import lax
import numpy as np

D_MODEL = 2048
BATCH = 16
SEQ = 2048
DEPTH = 1

HEAD_DIM = 128
ATTN_PATTERNS = ((128, 1), (512, 4), (2048, 16))
N_GROUPS = len(ATTN_PATTERNS)
HEADS_PER_GROUP = 4
ATTN_QKV = N_GROUPS * HEADS_PER_GROUP * HEAD_DIM
ATTN_OUT = HEADS_PER_GROUP * HEAD_DIM
BLK = 128
CONV_WIDTH = 1024
CONV_K = 3
MEM_LEN = 256
MEM_HEADS = 4
MEM_HEAD_DIM = 256
MEM_W = MEM_HEADS * MEM_HEAD_DIM
N_BRANCH = 3
EPS = 1e-6

SPLIT_SIZES = (ATTN_QKV, ATTN_QKV, ATTN_QKV, ATTN_OUT,
               CONV_WIDTH, CONV_WIDTH, CONV_WIDTH, CONV_WIDTH,
               MEM_W, MEM_W, N_BRANCH * D_MODEL)
IN_COLS = int(sum(SPLIT_SIZES))
SPLIT_IDX = [int(i) for i in np.cumsum(SPLIT_SIZES)[:-1]]

kernel_name = "hybrid_dilated_attn_shortconv_memory_gated_merge"


def rms_norm(t, g):
    tf = t.astype(jnp.float32)
    y = tf * lax.rsqrt(jnp.mean(tf * tf, axis=-1, keepdims=True) + EPS) * g.astype(jnp.float32)
    return y.astype(t.dtype)


def banded_causal_attn(q, k, v, span):
    N, L, H, E = q.shape
    nb = -(-L // BLK)
    pad = nb * BLK - L
    q = jnp.pad(q, ((0, 0), (0, pad), (0, 0), (0, 0)))
    k = jnp.pad(k, ((0, 0), (BLK, pad), (0, 0), (0, 0)))
    v = jnp.pad(v, ((0, 0), (BLK, pad), (0, 0), (0, 0)))
    qb = q.reshape(N, nb, BLK, H, E)

    def two_blocks(t):
        t = t.reshape(N, nb + 1, BLK, H, E)
        return jnp.concatenate([t[:, :-1], t[:, 1:]], axis=2)

    kb, vb = two_blocks(k), two_blocks(v)
    s = jnp.einsum('nbqhe,nbkhe->nbhqk', qb.astype(jnp.float32), kb.astype(jnp.float32)) * (E ** -0.5)
    qpos = jnp.arange(BLK)[:, None] + BLK
    kpos = jnp.arange(2 * BLK)[None, :]
    rel = qpos - kpos
    band = (rel >= 0) & (rel <= span)
    valid = (jnp.arange(nb)[:, None] * BLK + kpos - BLK) >= 0
    mask = band[None] & valid[:, None, :]
    s = jnp.where(mask[None, :, None], s, -jnp.inf)
    m = jnp.max(s, axis=-1, keepdims=True)
    p = jnp.exp(s - m)
    den = jnp.sum(p, axis=-1, keepdims=True)
    o = jnp.einsum('nbhqk,nbkhe->nbqhe', p / den, vb.astype(jnp.float32))
    lse = (m + jnp.log(den))[..., 0]
    o = o.reshape(N, nb * BLK, H, E)[:, :L]
    lse = lse.transpose(0, 1, 3, 2).reshape(N, nb * BLK, H)[:, :L]
    return o, lse


def dilated_causal_attn(q, k, v, window, dilation):
    B, S, H, E = q.shape
    L = S // dilation

    def to_classes(t):
        return t.reshape(B, L, dilation, H, E).transpose(0, 2, 1, 3, 4).reshape(B * dilation, L, H, E)

    o, lse = banded_causal_attn(to_classes(q), to_classes(k), to_classes(v), window // dilation)
    o = o.reshape(B, dilation, L, H, E).transpose(0, 2, 1, 3, 4).reshape(B, S, H, E)
    lse = lse.reshape(B, dilation, L, H).transpose(0, 2, 1, 3).reshape(B, S, H)
    return o, lse


def setup_inputs(seed: int = 0) -> dict:
    key = jax.random.key(seed)
    ks = jax.random.split(key, 16)
    f32 = jnp.float32

    def w(k, shape, fan_in):
        return jax.random.normal(k, shape, f32) * (fan_in ** -0.5)

    def gain(k, shape):
        return 1.0 + 0.02 * jax.random.normal(k, shape, f32)

    return {
        "x": jax.random.normal(ks[0], (BATCH, SEQ, D_MODEL), f32),
        "mem": jax.random.normal(ks[1], (BATCH, MEM_LEN, D_MODEL), f32),
        "norm_g": gain(ks[2], (D_MODEL,)),
        "mem_norm_g": gain(ks[3], (D_MODEL,)),
        "w_in": w(ks[4], (D_MODEL, IN_COLS), D_MODEL),
        "attn_q_norm": gain(ks[5], (N_GROUPS, HEAD_DIM)),
        "attn_k_norm": gain(ks[6], (N_GROUPS, HEAD_DIM)),
        "conv_w": w(ks[7], (CONV_K, CONV_WIDTH), CONV_K),
        "mem_w_kv": w(ks[8], (D_MODEL, 2 * MEM_W), D_MODEL),
        "mem_q_norm": gain(ks[9], (MEM_HEAD_DIM,)),
        "mem_k_norm": gain(ks[10], (MEM_HEAD_DIM,)),
        "w_br_attn": w(ks[11], (ATTN_OUT, D_MODEL), ATTN_OUT),
        "w_br_conv": w(ks[12], (CONV_WIDTH, D_MODEL), CONV_WIDTH),
        "w_br_mem": w(ks[13], (MEM_W, D_MODEL), MEM_W),
        "w_out": w(ks[14], (D_MODEL, D_MODEL), D_MODEL),
    }


def reference(x, mem, norm_g, mem_norm_g, w_in, attn_q_norm, attn_k_norm, conv_w,
              mem_w_kv, mem_q_norm, mem_k_norm, w_br_attn, w_br_conv, w_br_mem, w_out):
    B, S, D = x.shape
    for _layer in range(DEPTH):
        h = rms_norm(x, norm_g)
        proj = jnp.einsum('bsd,dc->bsc', h, w_in)
        (q, k, v, z_attn, conv_b, conv_c, conv_v, z_conv,
         mem_q, z_mem, gates) = jnp.split(proj, SPLIT_IDX, axis=-1)

        q = q.reshape(B, S, N_GROUPS, HEADS_PER_GROUP, HEAD_DIM)
        k = k.reshape(B, S, N_GROUPS, HEADS_PER_GROUP, HEAD_DIM)
        v = v.reshape(B, S, N_GROUPS, HEADS_PER_GROUP, HEAD_DIM)
        outs, lses = [], []
        for g, (window, dilation) in enumerate(ATTN_PATTERNS):
            qg = rms_norm(q[:, :, g], attn_q_norm[g])
            kg = rms_norm(k[:, :, g], attn_k_norm[g])
            o, lse = dilated_causal_attn(qg, kg, v[:, :, g], window, dilation)
            outs.append(o)
            lses.append(lse)
        alpha = jax.nn.softmax(jnp.stack(lses, axis=0), axis=0)
        a = jnp.sum(alpha[..., None] * jnp.stack(outs, axis=0), axis=0)
        a = a.reshape(B, S, ATTN_OUT).astype(x.dtype) * jax.nn.silu(z_attn)

        u = conv_c * conv_v
        y = sum(conv_w[j] * jnp.pad(u, ((0, 0), (j, 0), (0, 0)))[:, :S] for j in range(CONV_K))
        c = conv_b * y * jax.nn.silu(z_conv)

        mh = rms_norm(mem, mem_norm_g)
        mkv = jnp.einsum('bmd,dc->bmc', mh, mem_w_kv)
        mk, mv = jnp.split(mkv, 2, axis=-1)
        M = mem.shape[1]
        mq = rms_norm(mem_q.reshape(B, S, MEM_HEADS, MEM_HEAD_DIM), mem_q_norm)
        mk = rms_norm(mk.reshape(B, M, MEM_HEADS, MEM_HEAD_DIM), mem_k_norm)
        mv = mv.reshape(B, M, MEM_HEADS, MEM_HEAD_DIM)
        ms = jnp.einsum('bshe,bmhe->bhsm', mq.astype(jnp.float32), mk.astype(jnp.float32)) * (MEM_HEAD_DIM ** -0.5)
        mp = jax.nn.softmax(ms, axis=-1)
        mo = jnp.einsum('bhsm,bmhe->bshe', mp, mv.astype(jnp.float32))
        mo = mo.reshape(B, S, MEM_W).astype(x.dtype) * jax.nn.silu(z_mem)

        gt = jax.nn.sigmoid(gates.astype(jnp.float32).reshape(B, S, N_BRANCH, D)).astype(x.dtype)
        merged = (gt[:, :, 0] * jnp.einsum('bsc,cd->bsd', a, w_br_attn)
                  + gt[:, :, 1] * jnp.einsum('bsc,cd->bsd', c, w_br_conv)
                  + gt[:, :, 2] * jnp.einsum('bsc,cd->bsd', mo, w_br_mem))
        x = x + jnp.einsum('bsd,de->bse', merged, w_out)
    return x
```

```python
import contextlib
import numpy as np
import ml_dtypes
import concourse.bass as bass
import concourse.mybir as mybir
from concourse.bass_utils import run_bass_kernel_spmd

F32 = mybir.dt.float32
BF16 = mybir.dt.bfloat16
AF = mybir.ActivationFunctionType
ALU = mybir.AluOpType

S = 2048
D = 2048
NKC = 16
INC = 17408
EPS = 1e-6
NW = 6
DEPTH = NW - 1
C_Q, C_K, C_V, C_ZA = 0, 1536, 3072, 4608
C_CB, C_CC, C_CV, C_CZ = 5120, 6144, 7168, 8192
C_MQ, C_MZ, C_G = 9216, 10240, 11264
DIL = (1, 4, 16)


class _Op:
    __slots__ = ("eng", "fn", "reads", "writes", "is_dma", "token", "pre", "deps", "need_sig", "idx")


class _Chan:
    def __init__(self, sems):
        self.sems = sems
        self.n = 0


class Prog:
    ENGS = ("pe", "act", "dve", "pool", "sp")

    def __init__(self, dry):
        self.dry = dry
        self.ops = []

    def op(self, eng, fn, reads=(), writes=()):
        if self.dry:
            return
        o = _Op()
        o.eng, o.fn, o.reads, o.writes, o.is_dma = eng, fn, tuple(reads), tuple(writes), False
        o.token = None
        o.pre = None
        self.ops.append(o)

    def dma(self, eng, fn, chan, reads=(), writes=()):
        if self.dry:
            return
        o = _Op()
        o.eng, o.fn, o.reads, o.writes, o.is_dma = eng, fn, tuple(reads), tuple(writes), True
        n = chan.n
        chan.n += 1
        R = len(chan.sems)
        o.token = (chan.sems[n % R], 16 * (n // R + 1))
        o.pre = (chan.sems[n % R], 16 * (n // R)) if n >= R else None
        self.ops.append(o)

    def analyze(self, eng_sems, epoch=4096):
        last_w = {}
        readers = {}
        for i, o in enumerate(self.ops):
            o.idx = i
            o.need_sig = False
            deps = set()
            war = set()
            for b in o.reads:
                if b in last_w:
                    deps.add(last_w[b])
            for b in o.writes:
                if b in last_w:
                    deps.add(last_w[b])
                for r in readers.get(b, ()):
                    war.add(r)
            war -= deps
            deps.discard(i)
            war.discard(i)
            best = {}
            dma_deps = []
            for d in list(deps) + list(war):
                p = self.ops[d]
                if p.is_dma:
                    dma_deps.append(d)
                else:
                    if p.eng == o.eng and not o.is_dma:
                        if o.eng == "pe" or d in war:
                            continue
                    if p.eng not in best or best[p.eng] < d:
                        best[p.eng] = d
            o.deps = list(best.values()) + dma_deps
            for d in best.values():
                self.ops[d].need_sig = True
            for b in o.writes:
                last_w[b] = i
                readers[b] = []
            for b in o.reads:
                if b not in o.writes:
                    readers.setdefault(b, []).append(i)
        cnt = {e: 0 for e in self.ENGS}
        for o in self.ops:
            if o.need_sig and not o.is_dma:
                n = cnt[o.eng]
                cnt[o.eng] += 1
                sems = eng_sems[o.eng]
                assert n // epoch < len(sems), "not enough engine semaphores"
                o.token = (sems[n // epoch], n % epoch + 1)
        return cnt

    def emit(self, eng_name, eng):
        waited = {}
        for o in self.ops:
            if o.eng != eng_name:
                continue
            toks = [self.ops[d].token for d in o.deps]
            if o.pre is not None:
                toks.append(o.pre)
            for sem, val in toks:
                k = id(sem)
                if waited.get(k, 0) < val:
                    eng.wait_ge(sem, val)
                    waited[k] = val
            inst = o.fn(eng)
            if o.is_dma:
                inst.then_inc(o.token[0], 16)
            elif o.need_sig:
                inst.then_inc(o.token[0], 1)


def build(nseq=2, debug=False):
    nc = bass.Bass("TRN2", target_bir_lowering=False)

    def din(name, shape, dt=F32):
        return nc.dram_tensor(name, list(shape), dt, kind="ExternalInput").ap()

    x = din("x", [nseq, S, D])
    mem = din("mem", [nseq, 256, D])
    norm_g = din("norm_g", [D])
    mem_norm_g = din("mem_norm_g", [D])
    w_in = din("w_in", [D, INC])
    attn_q_norm = din("attn_q_norm", [3, 128])
    attn_k_norm = din("attn_k_norm", [3, 128])
    conv_w = din("conv_w", [3, 1024])
    mem_w_kv = din("mem_w_kv", [D, 2048])
    mem_q_norm = din("mem_q_norm", [256])
    mem_k_norm = din("mem_k_norm", [256])
    w_br_attn = din("w_br_attn", [512, D])
    w_br_conv = din("w_br_conv", [1024, D])
    w_br_mem = din("w_br_mem", [1024, D])
    w_out = din("w_out", [D, D])
    c_ident = din("c_ident", [128, 128], BF16)
    c_m01 = din("c_m01", [128, 512], BF16)
    c_m2 = din("c_m2", [128, 4 * 512], BF16)
    y = nc.dram_tensor("y", [nseq, S, D], F32, kind="ExternalOutput").ap()
    dbg = None
    if debug:
        dbg = nc.dram_tensor("dbg", [128, 36 * 2048], BF16, kind="ExternalOutput").ap()

    def dscr(name, nblk, nk):
        return nc.dram_tensor(name, [nblk, 128, nk, 128], BF16, kind="Internal").ap()

    ws = {
        "in": dscr("ws_in", 136, 16),
        "kv": dscr("ws_kv", 16, 16),
        "ba": dscr("ws_ba", 16, 4),
        "bc": dscr("ws_bc", 16, 8),
        "bm": dscr("ws_bm", 16, 8),
        "out": dscr("ws_out", 16, 16),
    }
    wsrc = {
        "in": w_in.rearrange("(kc p) (j c) -> j p kc c", p=128, c=128),
        "kv": mem_w_kv.rearrange("(kc p) (j c) -> j p kc c", p=128, c=128),
        "ba": w_br_attn.rearrange("(kc p) (j c) -> j p kc c", p=128, c=128),
        "bc": w_br_conv.rearrange("(kc p) (j c) -> j p kc c", p=128, c=128),
        "bm": w_br_mem.rearrange("(kc p) (j c) -> j p kc c", p=128, c=128),
        "out": w_out.rearrange("(kc p) (j c) -> j p kc c", p=128, c=128),
    }
    wnk = {"in": 16, "kv": 16, "ba": 4, "bc": 8, "bm": 8, "out": 16}

    es = contextlib.ExitStack()
    with es:
        def sb(name, shape, dt):
            return es.enter_context(nc.sbuf_tensor(name, list(shape), dt))

        hT = sb("hT", [128, 16, 2048], BF16)
        bo = sb("bo", [128, 20, 2048], BF16)
        Db = sb("Db", [128, 4, 2048], BF16)
        wp = sb("wp", [128, NW, 2048], BF16)
        tmp = sb("tmp", [128, 6, 512], F32)
        ident = sb("ident", [128, 128], BF16)
        ones = sb("ones", [128, 128], BF16)
        m01 = sb("m01", [128, 512], BF16)
        m2 = sb("m2", [128, 4, 512], BF16)
        gq = sb("gq", [128, 3], F32)
        gk = sb("gk", [128, 3], F32)
        gmq = sb("gmq", [128, 2], F32)
        gmk = sb("gmk", [128, 2], F32)
        cw = sb("cw", [128, 3, 8], F32)
        sml = sb("sml", [128, 16], F32)
        psb = [es.enter_context(nc.psum_tensor("ps%d" % i, [128, 512], F32)) for i in range(8)]

        def sem(name):
            return es.enter_context(nc.semaphore(name))

        eng_sems = {e: [sem("s_%s_%d" % (e, i)) for i in range(6)] for e in ("pe", "act", "dve", "pool")}
        eng_sems["sp"] = []
        ch_w = _Chan([sem("c_w%d" % i) for i in range(NW)])
        ch_cv = _Chan([sem("c_cv%d" % i) for i in range(8)])
        ch_x = _Chan([sem("c_x%d" % i) for i in range(4)])
        ch_st = _Chan([sem("c_st%d" % i) for i in range(4)])
        ch_c = _Chan([sem("c_c%d" % i) for i in range(4)])

        def emit_all(P, plan):
            ps_state = {"next": 0, "open": [False] * 8}

            def ps_alloc():
                i = ps_state["next"]
                ps_state["next"] = (i + 1) % 8
                assert not ps_state["open"][i], "psum bank %d still live" % i
                ps_state["open"][i] = True
                return i

            def ps_free(i):
                ps_state["open"][i] = False

            def PSB(i):
                return ("ps", i)

            def T(i):
                return tmp[:, i, :]

            def TB(i):
                return tmp[:, i, :].bitcast(BF16)

            def TK(i):
                return ("tmp", i)

            def BO(j):
                return ("bo", j)

            def mm(out, lhsT, rhs, start, stop, reads, writes):
                P.op("pe", lambda e: e.matmul(out, lhsT, rhs, start=start, stop=stop, skip_group_check=True),
                     reads, writes)

            def act(out, in_, func, reads, writes, scale=None, bias=None, accum_out=None):
                kw = {}
                if scale is not None:
                    kw["scale"] = scale
                if bias is not None:
                    kw["bias"] = bias
                if accum_out is not None:
                    kw["accum_out"] = accum_out
                P.op("act", lambda e: e.activation(out=out, in_=in_, func=func, **kw), reads, writes)

            def tt(eng, out, in0, in1, op, reads, writes):
                P.op(eng, lambda e: e.tensor_tensor(out=out, in0=in0, in1=in1, op=op), reads, writes)

            def stt(out, in0, scalar, in1, op0, op1, reads, writes):
                P.op("dve", lambda e: e.scalar_tensor_tensor(out=out, in0=in0, scalar=scalar, in1=in1,
                                                             op0=op0, op1=op1), reads, writes)

            def ts(eng, out, in0, scalar1, op0, reads, writes):
                P.op(eng, lambda e: e.tensor_scalar(out=out, in0=in0, scalar1=scalar1, scalar2=None, op0=op0),
                     reads, writes)

            def cp(eng, out, in_, reads, writes):
                if eng == "act":
                    P.op("act", lambda e: e.activation(out=out, in_=in_, func=AF.Copy), reads, writes)
                else:
                    P.op(eng, lambda e: e.tensor_copy(out=out, in_=in_), reads, writes)

            wst = {"req": [], "issued": 0, "consumed": 0, "conv": 0}
            CONV_AHEAD = 32

            def w_convert_upto(n):
                while wst["conv"] < min(len(plan), n):
                    for (name, j) in plan[wst["conv"]]:
                        convert(name, j)
                    wst["conv"] += 1

            def w_issue(n):
                desc = plan[n]
                slot = n % NW
                off = 0
                for (name, j) in desc:
                    nk = wnk[name]
                    dst = wp[:, slot, off * 128:(off + nk) * 128].rearrange("p (k c) -> p k c", c=128)
                    src = ws[name][j]
                    P.dma("sp", lambda e, dst=dst, src=src: e.dma_start(out=dst, in_=src), ch_w,
                          reads=[("ws", name, j)], writes=[("w", slot)])
                    off += nk

            def wget_group(descs):
                descs = [tuple(d) for d in descs]
                if P.dry:
                    wst["req"].extend(descs)
                    return [(None, None)] * len(descs)
                n0 = wst["consumed"]
                assert len(descs) <= NW
                for i, d in enumerate(descs):
                    assert plan[n0 + i] == d, (n0 + i, plan[n0 + i], d)
                w_convert_upto(n0 + CONV_AHEAD)
                while wst["issued"] < min(len(plan), n0 + NW):
                    w_issue(wst["issued"])
                    wst["issued"] += 1
                wst["consumed"] += len(descs)
                out = []
                for i in range(len(descs)):
                    slot = (n0 + i) % NW
                    out.append((wp[:, slot, :].rearrange("p (k c) -> p k c", c=128), ("w", slot)))
                return out

            def wget(*desc):
                return wget_group([desc])[0]

            conv_done = set()

            def convert(name, j):
                if (name, j) in conv_done:
                    return
                conv_done.add((name, j))
                dst = ws[name][j]
                src = wsrc[name][j]
                P.dma("pool", lambda e: e.dma_start(out=dst, in_=src), ch_cv, writes=[("ws", name, j)])

            if not P.dry:
                with nc.allow_non_contiguous_dma(reason="tiny const loads"):
                    def cdma(dst, src, key):
                        P.dma("sp", lambda e: e.dma_start(out=dst, in_=src, allow_slow_non_contiguous=True), ch_c, writes=[key])
                    cdma(ident[:], c_ident, "ident")
                    cdma(m01[:], c_m01, "m01")
                    cdma(m2[:].rearrange("p a b -> p (a b)"), c_m2, "m2")
                    cdma(gq[:], attn_q_norm.rearrange("g e -> e g"), "gq")
                    cdma(gk[:], attn_k_norm.rearrange("g e -> e g"), "gk")
                    cdma(gmq[:], mem_q_norm.rearrange("(e p) -> p e", p=128), "gmq")
                    cdma(gmk[:], mem_k_norm.rearrange("(e p) -> p e", p=128), "gmk")
                    cdma(cw[:], conv_w.rearrange("j (c p) -> p j c", p=128), "cw")
                P.op("dve", lambda e: e.memset(ones[:], 1.0), writes=["ones"])

            def norm_transpose(src_rows, ntile, gvec, dstT, dst_key, width):
                xs = [bo[:, 4:6, :].rearrange("p a b -> p (a b)").bitcast(F32),
                      bo[:, 6:8, :].rearrange("p a b -> p (a b)").bitcast(F32)]
                xk = [[BO(4), BO(5)], [BO(6), BO(7)]]
                gb = bo[:, 8:10, :].rearrange("p a b -> p (a b)").bitcast(F32)
                gbk = [BO(8), BO(9)]
                junk = bo[:, 10, :]
                hb = [bo[:, 11, :], bo[:, 12, :]]
                P.dma("sp", lambda e: e.dma_start(out=gb, in_=gvec.partition_broadcast(128)), ch_x, writes=gbk)
                for i in range(ntile):
                    s = i % 2
                    src = src_rows(i)
                    P.dma("sp", lambda e, s=s, src=src: e.dma_start(out=xs[s], in_=src), ch_x, writes=xk[s])
                    ssq = sml[:, s:s + 1]
                    lnv = sml[:, 2 + s:3 + s]
                    rstd = sml[:, 4 + s:5 + s]
                    act(junk, xs[s], AF.Square, xk[s], [BO(10), ("ssq", s)], accum_out=ssq)
                    act(lnv, ssq, AF.Ln, [("ssq", s)], [("lnv", s)], scale=1.0 / D, bias=EPS)
                    act(rstd, lnv, AF.Exp, [("lnv", s)], [("rstd", s)], scale=-0.5)
                    stt(hb[s], xs[s], rstd, gb, ALU.mult, ALU.mult, xk[s] + gbk + [("rstd", s)], [BO(11 + s)])
                    for half in range(2):
                        b_ = ps_alloc()
                        pv = psb[b_][:].bitcast(BF16)
                        for q in range(8):
                            kc = half * 8 + q
                            o_ = pv[:, q * 128:(q + 1) * 128]
                            i_ = hb[s][:, kc * 128:(kc + 1) * 128]
                            P.op("pe", lambda e, o_=o_, i_=i_: e.transpose(o_, i_, ident[:]),
                                 [BO(11 + s), "ident"], [PSB(b_)])
                        dst = dstT[:, half * 8:(half + 1) * 8, i * 128:(i + 1) * 128]
                        srcp = pv.rearrange("p (q t) -> p q t", t=128)
                        cp("act" if half == 0 else "dve", dst, srcp, [], [PSB(b_)] + dst_key)
                        ps_free(b_)

            def proj_fm(wv, wk, c):
                b_ = ps_alloc()
                for kc in range(NKC):
                    mm(psb[b_][:], wv[:, kc, :], hT[:, kc, c * 512:(c + 1) * 512], kc == 0, kc == NKC - 1,
                       [wk, "hT"], [PSB(b_)])
                return b_

            def rms_scale(banks, gains, outs, out_keys, ndim, t_sq, t_r):
                n = len(banks)
                sqv = TB(t_sq)
                for i, b_ in enumerate(banks):
                    act(sqv[:, i * 512:(i + 1) * 512], psb[b_][:], AF.Square, [], [PSB(b_), TK(t_sq)])
                sb_ = ps_alloc()
                for i in range(n):
                    mm(psb[sb_][:], ones[:], sqv[:, i * 512:(i + 1) * 512], i == 0, i == n - 1,
                       ["ones", TK(t_sq)], [PSB(sb_)])
                act(T(t_r), psb[sb_][:], AF.Ln, [], [PSB(sb_), TK(t_r)], scale=1.0 / ndim, bias=EPS)
                ps_free(sb_)
                act(T(t_r), T(t_r), AF.Exp, [], [TK(t_r)], scale=-0.5)
                for i, b_ in enumerate(banks):
                    stt(outs[i], psb[b_][:], gains[i], T(t_r), ALU.mult, ALU.mult,
                        [TK(t_r)] + gains_keys, [PSB(b_)] + out_keys[i])
                    ps_free(b_)

            gains_keys = ["gq", "gk", "gmq", "gmk"]
            ykeys = []
            MK = [("mT", k) for k in range(4)]
            MV = [("mT", k) for k in range(4, 8)]

            for si in range(nseq):
                norm_transpose(lambda i: x[si, i * 128:(i + 1) * 128, :], 16, norm_g, hT, ["hT"], 2048)

                mhT = bo[:, 13:15, :].rearrange("p a b -> p (a b)").rearrange("p (k t) -> p k t", t=256)
                norm_transpose(lambda i: mem[si, i * 128:(i + 1) * 128, :], 2, mem_norm_g, mhT, [BO(13), BO(14)], 256)
                mkT = Db[:, 0, :].rearrange("p (k t) -> p k t", t=256)
                mv = Db[:, 1, :].rearrange("p (m c) -> p m c", c=1024)
                for mh in range(4):
                    banks = []
                    for e in range(2):
                        wv, wk = wget(("kv", mh * 2 + e))
                        b_ = ps_alloc()
                        if not P.dry:
                            for kc in range(NKC):
                                mm(psb[b_][:, 0:256], wv[:, kc, :], mhT[:, kc, :], kc == 0, kc == NKC - 1,
                                   [wk, BO(13), BO(14)], [PSB(b_)])
                        banks.append(b_)
                    if not P.dry:
                        sqv = TB(0)
                        for e in range(2):
                            act(sqv[:, e * 256:(e + 1) * 256], psb[banks[e]][:, 0:256], AF.Square, [],
                                [PSB(banks[e]), TK(0)])
                        sb_ = ps_alloc()
                        for e in range(2):
                            mm(psb[sb_][:, 0:256], ones[:], sqv[:, e * 256:(e + 1) * 256], e == 0, e == 1,
                               ["ones", TK(0)], [PSB(sb_)])
                        act(T(1)[:, 0:256], psb[sb_][:, 0:256], AF.Ln, [], [PSB(sb_), TK(1)], scale=1.0 / 256, bias=EPS)
                        ps_free(sb_)
                        act(T(1)[:, 0:256], T(1)[:, 0:256], AF.Exp, [], [TK(1)], scale=-0.5)
                        for e in range(2):
                            stt(mkT[:, mh * 2 + e, :], psb[banks[e]][:, 0:256], gmk[:, e:e + 1], T(1)[:, 0:256],
                                ALU.mult, ALU.mult, [TK(1), "gmk"], [PSB(banks[e])] + MK)
                    for b_ in banks:
                        ps_free(b_)
                for nbg in range(2):
                    vb = [ps_alloc(), ps_alloc()]
                    for q in range(4):
                        wv, wk = wget(("kv", 8 + nbg * 4 + q))
                        if P.dry:
                            continue
                        for mt in range(2):
                            for kc in range(NKC):
                                mm(psb[vb[mt]][:, q * 128:(q + 1) * 128], mhT[:, kc, mt * 128:(mt + 1) * 128],
                                   wv[:, kc, :], q == 0 and kc == 0, q == 3 and kc == NKC - 1,
                                   [wk, BO(13), BO(14)], [PSB(vb[mt])])
                    for mt in range(2):
                        if not P.dry:
                            cp("act" if mt == 0 else "dve", mv[:, mt, nbg * 512:(nbg + 1) * 512], psb[vb[mt]][:],
                               [], [PSB(vb[mt])] + MV)
                        ps_free(vb[mt])

                for h in range(4):
                    qT = [bo[:, 4 + g, :] for g in range(3)]
                    kT = [bo[:, 7 + g, :] for g in range(3)]
                    Vt = [bo[:, 10 + g, :].rearrange("p (j e) -> p j e", e=128) for g in range(3)]
                    szc = bo[:, 13, :]
                    for g in range(3):
                        for which, cbase, dst, gn, gkey in ((0, C_Q, qT[g], gq, "gq"), (1, C_K, kT[g], gk, "gk")):
                            wv, wk = wget(("in", (cbase + g * 512 + h * 128) // 128))
                            if P.dry:
                                continue
                            for c in range(4):
                                b_ = proj_fm(wv, wk, c)
                                rms_scale([b_], [gn[:, g:g + 1]], [dst[:, c * 512:(c + 1) * 512]],
                                          [[BO((4 if which == 0 else 7) + g)]], 128, 0, 1)
                        wv, wk = wget(("in", (C_V + g * 512 + h * 128) // 128))
                        if P.dry:
                            continue
                        d = DIL[g]
                        nbg_ = 16 // d
                        for jg in range(4):
                            b_ = ps_alloc()
                            for jj in range(4):
                                j = jg * 4 + jj
                                r, blk = j // nbg_, j % nbg_
                                st = blk * 128 * d + r
                                for kc in range(NKC):
                                    mm(psb[b_][:, jj * 128:(jj + 1) * 128], hT[:, kc, st:st + 127 * d + 1:d],
                                       wv[:, kc, :], jj == 0 and kc == 0, jj == 3 and kc == NKC - 1,
                                       [wk, "hT"], [PSB(b_)])
                            cp("act" if jg % 2 == 0 else "dve", Vt[g][:, jg * 4:(jg + 1) * 4, :],
                               psb[b_][:].rearrange("p (j e) -> p j e", e=128), [], [PSB(b_), BO(10 + g)])
                            ps_free(b_)
                    wv, wk = wget(("in", (C_ZA + h * 128) // 128))
                    if P.dry:
                        continue
                    for c in range(4):
                        b_ = proj_fm(wv, wk, c)
                        act(szc[:, c * 512:(c + 1) * 512], psb[b_][:], AF.Silu, [], [PSB(b_), BO(13)])
                        ps_free(b_)

                    for c in range(4):
                        numb = ps_alloc()
                        denb = ps_alloc()
                        jobs = []
                        for p_ in range(2):
                            tiles = []
                            for qi in range(2):
                                b = 4 * c + 2 * p_ + qi
                                oc = slice((b - 4 * c) * 128, (b - 4 * c + 1) * 128)
                                qa = qT[0][:, b * 128:(b + 1) * 128]
                                if b >= 1:
                                    tiles.append((2 * qi, kT[0][:, (b - 1) * 128:b * 128], qa, Vt[0][:, b - 1, :], oc, 0))
                                tiles.append((2 * qi + 1, kT[0][:, b * 128:(b + 1) * 128], qa, Vt[0][:, b, :], oc, 0))
                            jobs.append((tiles, m01[:], "m01", 128))
                        for p_ in range(2):
                            tiles = []
                            for qi in range(2):
                                r = 2 * p_ + qi
                                oc = slice(r, 512, 4)
                                qa = qT[1][:, c * 512 + r:(c + 1) * 512:4]
                                if c >= 1:
                                    kb = c - 1
                                    tiles.append((2 * qi, kT[1][:, kb * 512 + r:(kb + 1) * 512:4], qa,
                                                  Vt[1][:, r * 4 + kb, :], oc, 1))
                                tiles.append((2 * qi + 1, kT[1][:, c * 512 + r:(c + 1) * 512:4], qa,
                                              Vt[1][:, r * 4 + c, :], oc, 1))
                            jobs.append((tiles, m01[:], "m01", 128))
                        tiles = []
                        for r in range(16):
                            tiles.append((r, kT[2][:, r:2048:16], qT[2][:, c * 512 + r:(c + 1) * 512:16],
                                          Vt[2][:, r, :], slice(r, 512, 16), 2))
                        jobs.append((tiles, m2[:, c, :], "m2", 32))

                        pend = None
                        nmm = sum(len(j[0]) for j in jobs)
                        done = [0]

                        def finish(job_state):
                            tiles, pslot, w_ = job_state
                            pv_ = TB(pslot)
                            for (slot, ka, qa, va, oc, g_) in tiles:
                                first = done[0] == 0
                                last = done[0] == nmm - 1
                                rhs = pv_[:, slot * w_:(slot + 1) * w_]
                                mm(psb[numb][:, oc], va, rhs, first, last, [TK(pslot), BO(10 + g_)], [PSB(numb)])
                                mm(psb[denb][:, oc], ones[:], rhs, first, last, [TK(pslot), "ones"], [PSB(denb)])
                                done[0] += 1

                        for ji, (tiles, mask, mkey, w_) in enumerate(jobs):
                            sb_ = ps_alloc()
                            pslot = 2 + (ji % 2)
                            for ti, (slot, ka, qa, va, oc, g_) in enumerate(tiles):
                                mm(psb[sb_][:, slot * w_:(slot + 1) * w_], ka, qa, ti == 0, ti == len(tiles) - 1,
                                   [BO(4 + g_), BO(7 + g_)], [PSB(sb_)])
                            act(TB(pslot)[:, 0:512], psb[sb_][:], AF.Exp, [], [PSB(sb_), TK(pslot)],
                                scale=128 ** -0.5)
                            ps_free(sb_)
                            tt("dve", TB(pslot)[:, 0:512], TB(pslot)[:, 0:512], mask, ALU.mult, [mkey], [TK(pslot)])
                            if pend is not None:
                                finish(pend)
                            pend = (tiles, pslot, w_)
                        finish(pend)
                        P.op("dve", lambda e, denb=denb: e.reciprocal(out=T(4), in_=psb[denb][:]), [], [PSB(denb), TK(4)])
                        ps_free(denb)
                        tt("dve", T(4), T(4), szc[:, c * 512:(c + 1) * 512], ALU.mult, [BO(13)], [TK(4)])
                        tt("dve", bo[:, h, c * 512:(c + 1) * 512], psb[numb][:], T(4), ALU.mult, [TK(4)],
                           [PSB(numb), BO(h)])
                        ps_free(numb)

                uf = bo[:, 12:15, :].rearrange("p a b -> p (a b)").bitcast(F32)
                ukeys = [BO(12), BO(13), BO(14)]
                if not P.dry:
                    P.op("pool", lambda e: e.memset(uf[:, 0:2], 0.0), [], ukeys)
                for j in range(8):
                    (wB, kB), (wC, kC), (wV, kV), (wZ, kZ) = wget_group(
                        [[("in", (cb_ + j * 128) // 128)] for cb_ in (C_CB, C_CC, C_CV, C_CZ)])
                    if P.dry:
                        continue
                    for c in range(4):
                        bC = proj_fm(wC, kC, c)
                        bV = proj_fm(wV, kV, c)
                        bB = proj_fm(wB, kB, c)
                        bZ = proj_fm(wZ, kZ, c)
                        cs = slice(c * 512, (c + 1) * 512)
                        cp("act", T(0), psb[bC][:], [], [PSB(bC), TK(0)])
                        ps_free(bC)
                        tt("dve", uf[:, 2 + c * 512:2 + (c + 1) * 512], psb[bV][:], T(0), ALU.mult, [TK(0)],
                           [PSB(bV)] + ukeys)
                        ps_free(bV)
                        ts("pool", T(1), uf[:, 2 + c * 512:2 + (c + 1) * 512], cw[:, 0, j:j + 1], ALU.mult,
                           ukeys + ["cw"], [TK(1)])
                        stt(T(1), uf[:, 1 + c * 512:1 + (c + 1) * 512], cw[:, 1, j:j + 1], T(1), ALU.mult, ALU.add,
                            ukeys + ["cw"], [TK(1)])
                        stt(T(1), uf[:, c * 512:(c + 1) * 512], cw[:, 2, j:j + 1], T(1), ALU.mult, ALU.add,
                            ukeys + ["cw"], [TK(1)])
                        act(T(2), psb[bZ][:], AF.Silu, [], [PSB(bZ), TK(2)])
                        ps_free(bZ)
                        tt("dve", T(2), psb[bB][:], T(2), ALU.mult, [], [PSB(bB), TK(2)])
                        ps_free(bB)
                        tt("pool", bo[:, 4 + j, cs], T(1), T(2), ALU.mult, [TK(1), TK(2)], [BO(4 + j)])

                for mh in range(4):
                    wqz = wget_group([[("in", (cb_ + mh * 256 + e * 128) // 128)] for cb_ in (C_MQ, C_MZ)
                                      for e in range(2)])
                    wq, wz = wqz[0:2], wqz[2:4]
                    if P.dry:
                        continue
                    for c in range(4):
                        cs = slice(c * 512, (c + 1) * 512)
                        qb = [proj_fm(wq[e][0], wq[e][1], c) for e in range(2)]
                        mq = TB(2)
                        rms_scale(qb, [gmq[:, 0:1], gmq[:, 1:2]], [mq[:, 0:512], mq[:, 512:1024]],
                                  [[TK(2)], [TK(2)]], 256, 0, 1)
                        stb = [ps_alloc(), ps_alloc()]
                        for mt in range(2):
                            for e in range(2):
                                mm(psb[stb[mt]][:], mkT[:, mh * 2 + e, mt * 128:(mt + 1) * 128],
                                   mq[:, e * 512:(e + 1) * 512], e == 0, e == 1, MK + [TK(2)], [PSB(stb[mt])])
                        pT = TB(0)
                        for mt in range(2):
                            act(pT[:, mt * 512:(mt + 1) * 512], psb[stb[mt]][:], AF.Exp, [], [PSB(stb[mt]), TK(0)],
                                scale=256 ** -0.5)
                            ps_free(stb[mt])
                        mob = [ps_alloc(), ps_alloc()]
                        dnb = ps_alloc()
                        for e in range(2):
                            for mt in range(2):
                                mm(psb[mob[e]][:], mv[:, mt, mh * 256 + e * 128:mh * 256 + (e + 1) * 128],
                                   pT[:, mt * 512:(mt + 1) * 512], mt == 0, mt == 1, MV + [TK(0)], [PSB(mob[e])])
                        for mt in range(2):
                            mm(psb[dnb][:], ones[:], pT[:, mt * 512:(mt + 1) * 512], mt == 0, mt == 1,
                               ["ones", TK(0)], [PSB(dnb)])
                        zb = [proj_fm(wz[e][0], wz[e][1], c) for e in range(2)]
                        P.op("dve", lambda e, dnb=dnb: e.reciprocal(out=T(1), in_=psb[dnb][:]), [], [PSB(dnb), TK(1)])
                        ps_free(dnb)
                        for e in range(2):
                            act(T(3 + e), psb[zb[e]][:], AF.Silu, [], [PSB(zb[e]), TK(3 + e)])
                            ps_free(zb[e])
                            tt("pool", T(3 + e), T(3 + e), T(1), ALU.mult, [TK(1)], [TK(3 + e)])
                            tt("dve", bo[:, 12 + mh * 2 + e, cs], psb[mob[e]][:], T(3 + e), ALU.mult, [TK(3 + e)],
                               [PSB(mob[e]), BO(12 + mh * 2 + e)])
                            ps_free(mob[e])

                if debug and si == 0 and not P.dry:
                    P.dma("sp", lambda e: e.dma_start(out=dbg[:, 0:16 * 2048], in_=hT[:].rearrange("p a b -> p (a b)")),
                          ch_st, reads=["hT"], writes=["dbg0"])
                    P.dma("sp", lambda e: e.dma_start(out=dbg[:, 16 * 2048:36 * 2048],
                                                      in_=bo[:].rearrange("p a b -> p (a b)")),
                          ch_st, reads=[BO(j) for j in range(20)], writes=["dbg1"])

                mT = Db[:].rearrange("p a b -> p (a b)").rearrange("p (k t) -> p k t", t=512)
                for c in range(4):
                    cs = slice(c * 512, (c + 1) * 512)
                    for dc in range(16):
                        gb_ = []
                        for br in range(3):
                            wv, wk = wget(("in", (C_G + br * 2048 + dc * 128) // 128))
                            if not P.dry:
                                gb_.append(proj_fm(wv, wk, c))
                        wA, kA = wget(("ba", dc), ("bc", dc))
                        ab = [ps_alloc(), ps_alloc(), ps_alloc()]
                        if not P.dry:
                            for k in range(4):
                                mm(psb[ab[0]][:], wA[:, k, :], bo[:, k, cs], k == 0, k == 3, [kA, BO(k)], [PSB(ab[0])])
                            for k in range(8):
                                mm(psb[ab[1]][:], wA[:, 4 + k, :], bo[:, 4 + k, cs], k == 0, k == 7, [kA, BO(4 + k)],
                                   [PSB(ab[1])])
                        wM, kM = wget(("bm", dc))
                        if not P.dry:
                            for k in range(8):
                                mm(psb[ab[2]][:], wM[:, k, :], bo[:, 12 + k, cs], k == 0, k == 7, [kM, BO(12 + k)],
                                   [PSB(ab[2])])
                            for br in range(3):
                                act(T(br), psb[gb_[br]][:], AF.Sigmoid, [], [PSB(gb_[br]), TK(br)])
                                ps_free(gb_[br])
                                tt("dve", T(br), psb[ab[br]][:], T(br), ALU.mult, [], [PSB(ab[br]), TK(br)])
                            tt("pool", T(0), T(0), T(1), ALU.add, [TK(1)], [TK(0)])
                            tt("pool", mT[:, dc, :], T(0), T(2), ALU.add, [TK(0), TK(2)], [("mT", dc)])
                        for br in range(3):
                            ps_free(ab[br])
                    for nbk in range(4):
                        ob = [ps_alloc() for _ in range(4)]
                        for q in range(4):
                            wv, wk = wget(("out", nbk * 4 + q))
                            if P.dry:
                                continue
                            for tq in range(4):
                                for kc in range(NKC):
                                    mm(psb[ob[tq]][:, q * 128:(q + 1) * 128], mT[:, kc, tq * 128:(tq + 1) * 128],
                                       wv[:, kc, :], kc == 0 and q == 0, kc == NKC - 1 and q == 3,
                                       [wk, ("mT", kc)], [PSB(ob[tq])])
                        for tq in range(4):
                            if not P.dry:
                                rows = slice(c * 512 + tq * 128, c * 512 + (tq + 1) * 128)
                                cols = slice(nbk * 512, (nbk + 1) * 512)
                                xs_ = 3 + (tq % 2)
                                P.dma("sp", lambda e, xs_=xs_, rows=rows, cols=cols, si=si: e.dma_start(out=T(xs_), in_=x[si, rows, cols]),
                                      ch_x, writes=[TK(xs_)])
                                tt("dve", T(xs_), psb[ob[tq]][:], T(xs_), ALU.add, [], [PSB(ob[tq]), TK(xs_)])
                                P.dma("sp", lambda e, xs_=xs_, rows=rows, cols=cols, si=si: e.dma_start(out=y[si, rows, cols], in_=T(xs_)),
                                      ch_st, reads=[TK(xs_)], writes=[("y", len(ykeys))])
                                ykeys.append(("y", len(ykeys)))
                            ps_free(ob[tq])
            if not P.dry:
                P.op("sp", lambda e: e.nop(), reads=ykeys + ["dbg0", "dbg1"], writes=[])
            return wst["req"]

        plan = emit_all(Prog(dry=True), None)
        P = Prog(dry=False)
        emit_all(P, plan)
        P.analyze(eng_sems)
        with nc.Block() as block:
            @block.tensor
            def _(e):
                P.emit("pe", e)

            @block.scalar
            def _(e):
                P.emit("act", e)

            @block.vector
            def _(e):
                P.emit("dve", e)

            @block.gpsimd
            def _(e):
                P.emit("pool", e)

            @block.sync
            def _(e):
                P.emit("sp", e)
    return nc


def _consts():
    bf = ml_dtypes.bfloat16
    k = np.arange(128)[:, None]
    q = np.arange(128)[None, :]
    prev = (k >= q).astype(np.float32)
    cur = (k <= q).astype(np.float32)
    m01 = np.concatenate([prev, cur, prev, cur], axis=1)
    m2 = np.zeros((128, 4, 16, 32), np.float32)
    qi = np.arange(32)[None, :]
    for c in range(4):
        m2[:, c, :, :] = (k <= 32 * c + qi).astype(np.float32)[:, None, :]
    return {
        "c_ident": np.eye(128, dtype=np.float32).astype(bf),
        "c_m01": m01.astype(bf),
        "c_m2": m2.reshape(128, 4 * 512).astype(bf),
    }


_WNAMES = ("norm_g", "mem_norm_g", "w_in", "attn_q_norm", "attn_k_norm", "conv_w", "mem_w_kv",
           "mem_q_norm", "mem_k_norm", "w_br_attn", "w_br_conv", "w_br_mem", "w_out")


def kernel(**inputs):
    ncores = 8
    x = np.ascontiguousarray(np.asarray(inputs["x"], dtype=np.float32))
    mem = np.ascontiguousarray(np.asarray(inputs["mem"], dtype=np.float32))
    B = x.shape[0]
    nseq = B // ncores
    shared = {n: np.ascontiguousarray(np.asarray(inputs[n], dtype=np.float32)) for n in _WNAMES}
    shared.update(_consts())
    nc = build(nseq=nseq)
    in_maps = []
    for i in range(ncores):
        m = dict(shared)
        m["x"] = x[i * nseq:(i + 1) * nseq]
        m["mem"] = mem[i * nseq:(i + 1) * nseq]
        in_maps.append(m)
    res = run_bass_kernel_spmd(nc, in_maps, core_ids=list(range(ncores)))
    return np.concatenate([np.asarray(r["y"], dtype=np.float32) for r in res.results], axis=0)
```

```python
import contextlib
import numpy as np
import ml_dtypes
import concourse.bass as bass
import concourse.mybir as mybir
from concourse.bass_utils import run_bass_kernel_spmd

F32 = mybir.dt.float32
BF16 = mybir.dt.bfloat16
AF = mybir.ActivationFunctionType
ALU = mybir.AluOpType

S = 2048
D = 2048
NKC = 16
INC = 17408
EPS = 1e-6
NW = 6
DEPTH = NW - 1
C_Q, C_K, C_V, C_ZA = 0, 1536, 3072, 4608
C_CB, C_CC, C_CV, C_CZ = 5120, 6144, 7168, 8192
C_MQ, C_MZ, C_G = 9216, 10240, 11264
DIL = (1, 4, 16)


class _Op:
    __slots__ = ("eng", "fn", "reads", "writes", "is_dma", "token", "pre", "deps", "need_sig", "idx")


class _Chan:
    def __init__(self, sems):
        self.sems = sems
        self.n = 0


class Prog:
    ENGS = ("pe", "act", "dve", "pool", "sp")

    def __init__(self, dry):
        self.dry = dry
        self.ops = []

    def op(self, eng, fn, reads=(), writes=()):
        if self.dry:
            return
        o = _Op()
        o.eng, o.fn, o.reads, o.writes, o.is_dma = eng, fn, tuple(reads), tuple(writes), False
        o.token = None
        o.pre = None
        self.ops.append(o)

    def dma(self, eng, fn, chan, reads=(), writes=()):
        if self.dry:
            return
        o = _Op()
        o.eng, o.fn, o.reads, o.writes, o.is_dma = eng, fn, tuple(reads), tuple(writes), True
        n = chan.n
        chan.n += 1
        R = len(chan.sems)
        o.token = (chan.sems[n % R], 16 * (n // R + 1))
        o.pre = (chan.sems[n % R], 16 * (n // R)) if n >= R else None
        self.ops.append(o)

    def analyze(self, eng_sems, epoch=4096):
        last_w = {}
        readers = {}
        for i, o in enumerate(self.ops):
            o.idx = i
            o.need_sig = False
            deps = set()
            war = set()
            for b in o.reads:
                if b in last_w:
                    deps.add(last_w[b])
            for b in o.writes:
                if b in last_w:
                    deps.add(last_w[b])
                for r in readers.get(b, ()):
                    war.add(r)
            war -= deps
            deps.discard(i)
            war.discard(i)
            best = {}
            dma_deps = []
            for d in list(deps) + list(war):
                p = self.ops[d]
                if p.is_dma:
                    dma_deps.append(d)
                else:
                    if p.eng == o.eng and not o.is_dma:
                        if o.eng == "pe" or (d in war and o.eng != "pool"):
                            continue
                    if p.eng not in best or best[p.eng] < d:
                        best[p.eng] = d
            o.deps = list(best.values()) + dma_deps
            for d in best.values():
                self.ops[d].need_sig = True
            for b in o.writes:
                last_w[b] = i
                readers[b] = []
            for b in o.reads:
                if b not in o.writes:
                    readers.setdefault(b, []).append(i)
        cnt = {e: 0 for e in self.ENGS}
        for o in self.ops:
            if o.need_sig and not o.is_dma:
                n = cnt[o.eng]
                cnt[o.eng] += 1
                sems = eng_sems[o.eng]
                assert n // epoch < len(sems), "not enough engine semaphores"
                o.token = (sems[n // epoch], n % epoch + 1)
        return cnt

    def emit(self, eng_name, eng):
        waited = {}
        for o in self.ops:
            if o.eng != eng_name:
                continue
            toks = [self.ops[d].token for d in o.deps]
            if o.pre is not None:
                toks.append(o.pre)
            for sem, val in toks:
                k = id(sem)
                if waited.get(k, 0) < val:
                    eng.wait_ge(sem, val)
                    waited[k] = val
            inst = o.fn(eng)
            if o.is_dma:
                inst.then_inc(o.token[0], 16)
            elif o.need_sig:
                inst.then_inc(o.token[0], 1)


def build(nseq=2, debug=False):
    nc = bass.Bass("TRN2", target_bir_lowering=False)

    def din(name, shape, dt=F32):
        return nc.dram_tensor(name, list(shape), dt, kind="ExternalInput").ap()

    x = din("x", [nseq, S, D])
    mem = din("mem", [nseq, 256, D])
    norm_g = din("norm_g", [D])
    mem_norm_g = din("mem_norm_g", [D])
    w_in = din("w_in", [D, INC])
    attn_q_norm = din("attn_q_norm", [3, 128])
    attn_k_norm = din("attn_k_norm", [3, 128])
    conv_w = din("conv_w", [3, 1024])
    mem_w_kv = din("mem_w_kv", [D, 2048])
    mem_q_norm = din("mem_q_norm", [256])
    mem_k_norm = din("mem_k_norm", [256])
    w_br_attn = din("w_br_attn", [512, D])
    w_br_conv = din("w_br_conv", [1024, D])
    w_br_mem = din("w_br_mem", [1024, D])
    w_out = din("w_out", [D, D])
    c_ident = din("c_ident", [128, 128], BF16)
    c_m01 = din("c_m01", [128, 512], BF16)
    c_m2 = din("c_m2", [128, 4 * 512], BF16)
    y = nc.dram_tensor("y", [nseq, S, D], F32, kind="ExternalOutput").ap()
    dbg = None
    if debug:
        dbg = nc.dram_tensor("dbg", [128, 36 * 2048], BF16, kind="ExternalOutput").ap()

    def dscr(name, nblk, nk):
        return nc.dram_tensor(name, [nblk, 128, nk, 128], BF16, kind="Internal").ap()

    ws = {
        "in": dscr("ws_in", 136, 16),
        "kv": dscr("ws_kv", 16, 16),
        "ba": dscr("ws_ba", 16, 4),
        "bc": dscr("ws_bc", 16, 8),
        "bm": dscr("ws_bm", 16, 8),
        "out": dscr("ws_out", 16, 16),
    }
    wsrc = {
        "in": w_in.rearrange("(kc p) (j c) -> j p kc c", p=128, c=128),
        "kv": mem_w_kv.rearrange("(kc p) (j c) -> j p kc c", p=128, c=128),
        "ba": w_br_attn.rearrange("(kc p) (j c) -> j p kc c", p=128, c=128),
        "bc": w_br_conv.rearrange("(kc p) (j c) -> j p kc c", p=128, c=128),
        "bm": w_br_mem.rearrange("(kc p) (j c) -> j p kc c", p=128, c=128),
        "out": w_out.rearrange("(kc p) (j c) -> j p kc c", p=128, c=128),
    }
    wnk = {"in": 16, "kv": 16, "ba": 4, "bc": 8, "bm": 8, "out": 16}

    es = contextlib.ExitStack()
    with es:
        def sb(name, shape, dt):
            return es.enter_context(nc.sbuf_tensor(name, list(shape), dt))

        hT = sb("hT", [128, 16, 2048], BF16)
        bo = sb("bo", [128, 20, 2048], BF16)
        Db = sb("Db", [128, 4, 2048], BF16)
        wp = sb("wp", [128, NW, 2048], BF16)
        tmp = sb("tmp", [128, 6, 512], F32)
        ident = sb("ident", [128, 128], BF16)
        ones = sb("ones", [128, 128], BF16)
        m01 = sb("m01", [128, 512], BF16)
        m2 = sb("m2", [128, 4, 512], BF16)
        gq = sb("gq", [128, 3], F32)
        gk = sb("gk", [128, 3], F32)
        gmq = sb("gmq", [128, 2], F32)
        gmk = sb("gmk", [128, 2], F32)
        cw = sb("cw", [128, 3, 8], F32)
        sml = sb("sml", [128, 16], F32)
        psb = [es.enter_context(nc.psum_tensor("ps%d" % i, [128, 512], F32)) for i in range(8)]

        def sem(name):
            return es.enter_context(nc.semaphore(name))

        eng_sems = {e: [sem("s_%s_%d" % (e, i)) for i in range(6)] for e in ("pe", "act", "dve", "pool")}
        eng_sems["sp"] = []
        ch_w = _Chan([sem("c_w%d" % i) for i in range(NW)])
        ch_cv = _Chan([sem("c_cv%d" % i) for i in range(8)])
        ch_x = _Chan([sem("c_x%d" % i) for i in range(4)])
        ch_st = _Chan([sem("c_st%d" % i) for i in range(4)])
        ch_c = _Chan([sem("c_c%d" % i) for i in range(4)])

        def emit_all(P, plan):
            ps_state = {"next": 0, "open": [False] * 8}

            def ps_alloc():
                i = ps_state["next"]
                for _ in range(8):
                    if not ps_state["open"][i]:
                        break
                    i = (i + 1) % 8
                assert not ps_state["open"][i], "all psum banks live"
                ps_state["next"] = (i + 1) % 8
                ps_state["open"][i] = True
                return i

            def ps_free(i):
                ps_state["open"][i] = False

            def PSB(i):
                return ("ps", i)

            def T(i):
                return tmp[:, i, :]

            def TB(i):
                return tmp[:, i, :].bitcast(BF16)

            def TK(i):
                return ("tmp", i)

            def BO(j):
                return ("bo", j)

            def mm(out, lhsT, rhs, start, stop, reads, writes):
                P.op("pe", lambda e: e.matmul(out, lhsT, rhs, start=start, stop=stop, skip_group_check=True),
                     reads, writes)

            def act(out, in_, func, reads, writes, scale=None, bias=None, accum_out=None):
                kw = {}
                if scale is not None:
                    kw["scale"] = scale
                if bias is not None:
                    kw["bias"] = bias
                if accum_out is not None:
                    kw["accum_out"] = accum_out
                P.op("act", lambda e: e.activation(out=out, in_=in_, func=func, **kw), reads, writes)

            def tt(eng, out, in0, in1, op, reads, writes):
                P.op(eng, lambda e: e.tensor_tensor(out=out, in0=in0, in1=in1, op=op), reads, writes)

            def stt(out, in0, scalar, in1, op0, op1, reads, writes):
                P.op("dve", lambda e: e.scalar_tensor_tensor(out=out, in0=in0, scalar=scalar, in1=in1,
                                                             op0=op0, op1=op1), reads, writes)

            def ts(eng, out, in0, scalar1, op0, reads, writes):
                P.op(eng, lambda e: e.tensor_scalar(out=out, in0=in0, scalar1=scalar1, scalar2=None, op0=op0),
                     reads, writes)

            def cp(eng, out, in_, reads, writes):
                if eng == "act":
                    P.op("act", lambda e: e.activation(out=out, in_=in_, func=AF.Copy), reads, writes)
                else:
                    P.op(eng, lambda e: e.tensor_copy(out=out, in_=in_), reads, writes)

            wst = {"req": [], "issued": 0, "consumed": 0, "conv": 0}
            CONV_AHEAD = 32

            def w_convert_upto(n):
                while wst["conv"] < min(len(plan), n):
                    for (name, j) in plan[wst["conv"]]:
                        convert(name, j)
                    wst["conv"] += 1

            def w_issue(n):
                desc = plan[n]
                slot = n % NW
                off = 0
                for (name, j) in desc:
                    nk = wnk[name]
                    dst = wp[:, slot, off * 128:(off + nk) * 128].rearrange("p (k c) -> p k c", c=128)
                    src = ws[name][j]
                    P.dma("sp", lambda e, dst=dst, src=src: e.dma_start(out=dst, in_=src), ch_w,
                          reads=[("ws", name, j)], writes=[("w", slot)])
                    off += nk

            def wget_group(descs):
                descs = [tuple(d) for d in descs]
                if P.dry:
                    wst["req"].extend(descs)
                    return [(None, None)] * len(descs)
                n0 = wst["consumed"]
                assert len(descs) <= NW
                for i, d in enumerate(descs):
                    assert plan[n0 + i] == d, (n0 + i, plan[n0 + i], d)
                w_convert_upto(n0 + CONV_AHEAD)
                while wst["issued"] < min(len(plan), n0 + NW):
                    w_issue(wst["issued"])
                    wst["issued"] += 1
                wst["consumed"] += len(descs)
                out = []
                for i in range(len(descs)):
                    slot = (n0 + i) % NW
                    out.append((wp[:, slot, :].rearrange("p (k c) -> p k c", c=128), ("w", slot)))
                return out

            def wget(*desc):
                return wget_group([desc])[0]

            conv_done = set()

            def convert(name, j):
                if (name, j) in conv_done:
                    return
                conv_done.add((name, j))
                dst = ws[name][j]
                src = wsrc[name][j]
                P.dma("pool", lambda e: e.dma_start(out=dst, in_=src), ch_cv, writes=[("ws", name, j)])

            if not P.dry:
                with nc.allow_non_contiguous_dma(reason="tiny const loads"):
                    def cdma(dst, src, key):
                        P.dma("sp", lambda e: e.dma_start(out=dst, in_=src, allow_slow_non_contiguous=True), ch_c, writes=[key])
                    cdma(ident[:], c_ident, "ident")
                    cdma(m01[:], c_m01, "m01")
                    cdma(m2[:].rearrange("p a b -> p (a b)"), c_m2, "m2")
                    cdma(gq[:], attn_q_norm.rearrange("g e -> e g"), "gq")
                    cdma(gk[:], attn_k_norm.rearrange("g e -> e g"), "gk")
                    cdma(gmq[:], mem_q_norm.rearrange("(e p) -> p e", p=128), "gmq")
                    cdma(gmk[:], mem_k_norm.rearrange("(e p) -> p e", p=128), "gmk")
                    cdma(cw[:], conv_w.rearrange("j (c p) -> p j c", p=128), "cw")
                P.op("dve", lambda e: e.memset(ones[:], 1.0), writes=["ones"])

            def norm_transpose(src_rows, ntile, gvec, dstT, dst_keys, width):
                xs = [bo[:, 4 + 2 * k:6 + 2 * k, :].rearrange("p a b -> p (a b)").bitcast(F32) for k in range(3)]
                xk = [[BO(4 + 2 * k), BO(5 + 2 * k)] for k in range(3)]
                gb = bo[:, 10:12, :].rearrange("p a b -> p (a b)").bitcast(F32)
                gbk = [BO(10), BO(11)]
                junk = bo[:, 12, :]
                hb = [bo[:, 15, :], bo[:, 16, :]]
                hbk = [BO(15), BO(16)]
                P.dma("sp", lambda e: e.dma_start(out=gb, in_=gvec.partition_broadcast(128)), ch_x, writes=gbk)

                def load(i):
                    s3 = i % 3
                    src = src_rows(i)
                    P.dma("sp", lambda e, s3=s3, src=src: e.dma_start(out=xs[s3], in_=src), ch_x, writes=xk[s3])

                def stats(i):
                    s3, s = i % 3, i % 2
                    ssq = sml[:, s:s + 1]
                    lnv = sml[:, 2 + s:3 + s]
                    rstd = sml[:, 4 + s:5 + s]
                    act(junk, xs[s3], AF.Square, xk[s3], [BO(12), ("ssq", s)], accum_out=ssq)
                    act(lnv, ssq, AF.Ln, [("ssq", s)], [("lnv", s)], scale=1.0 / D, bias=EPS)
                    act(rstd, lnv, AF.Exp, [("lnv", s)], [("rstd", s)], scale=-0.5)

                def finish(i):
                    s3, s = i % 3, i % 2
                    rstd = sml[:, 4 + s:5 + s]
                    stt(hb[s], xs[s3], rstd, gb, ALU.mult, ALU.mult, xk[s3] + gbk + [("rstd", s)], [hbk[s]])
                    for half in range(2):
                        b_ = ps_alloc()
                        pv = psb[b_][:].bitcast(BF16)
                        for q in range(8):
                            kc = half * 8 + q
                            o_ = pv[:, q * 128:(q + 1) * 128]
                            i_ = hb[s][:, kc * 128:(kc + 1) * 128]
                            P.op("pe", lambda e, o_=o_, i_=i_: e.transpose(o_, i_, ident[:]),
                                 [hbk[s], "ident"], [PSB(b_)])
                        dst = dstT[:, half * 8:(half + 1) * 8, i * 128:(i + 1) * 128]
                        srcp = pv.rearrange("p (q t) -> p q t", t=128)
                        cp("act" if half == 0 else "dve", dst, srcp, [], [PSB(b_)] + dst_keys[half])
                        ps_free(b_)

                for i in range(min(2, ntile)):
                    load(i)
                for i in range(ntile + 1):
                    if i < ntile:
                        stats(i)
                    if i >= 1:
                        finish(i - 1)
                    if i + 2 < ntile:
                        load(i + 2)

            def proj_fm(wv, wk, c):
                b_ = ps_alloc()
                for kc in range(NKC):
                    mm(psb[b_][:], wv[:, kc, :], hT[:, kc, c * 512:(c + 1) * 512], kc == 0, kc == NKC - 1,
                       [wk, "hT.0", "hT.1"], [PSB(b_)])
                return b_

            def rms_a(banks, t_sq):
                sqv = TB(t_sq)
                for i, b_ in enumerate(banks):
                    act(sqv[:, i * 512:(i + 1) * 512], psb[b_][:], AF.Square, [], [PSB(b_), TK(t_sq)])

            def rms_b1(banks, t_sq):
                sqv = TB(t_sq)
                n = len(banks)
                sb_ = ps_alloc()
                for i in range(n):
                    mm(psb[sb_][:], ones[:], sqv[:, i * 512:(i + 1) * 512], i == 0, i == n - 1,
                       ["ones", TK(t_sq)], [PSB(sb_)])
                return sb_

            def rms_b2(banks, sb_, gains, outs, out_keys, ndim, t_r):
                act(T(t_r), psb[sb_][:], AF.Ln, [], [PSB(sb_), TK(t_r)], scale=1.0 / ndim, bias=EPS)
                ps_free(sb_)
                act(T(t_r), T(t_r), AF.Exp, [], [TK(t_r)], scale=-0.5)
                for i, b_ in enumerate(banks):
                    stt(outs[i], psb[b_][:], gains[i], T(t_r), ALU.mult, ALU.mult,
                        [TK(t_r)] + gains_keys, [PSB(b_)] + out_keys[i])
                    ps_free(b_)

            pend = []

            def flush_pending():
                todo = pend[:]
                del pend[:]
                for f in todo:
                    f()

            rr = {"n": 0}

            def qk_chunk(wv, wk, c, gain, out, out_key):
                b_ = proj_fm(wv, wk, c)
                flush_pending()
                k = rr["n"] % 2
                rr["n"] += 1
                t_sq, t_r = (0, 1) if k == 0 else (5, 4)
                rms_a([b_], t_sq)

                def fin():
                    sb_ = rms_b1([b_], t_sq)
                    rms_b2([b_], sb_, [gain], [out], [out_key], 128, t_r)
                pend.append(fin)

            gains_keys = ["gq", "gk", "gmq", "gmk"]
            ykeys = []
            MK = [("mT", k) for k in range(4)]
            MV = [("mT", k) for k in range(4, 8)]

            for si in range(nseq):
                norm_transpose(lambda i: x[si, i * 128:(i + 1) * 128, :], 16, norm_g, hT, [["hT.0"], ["hT.1"]], 2048)

                mhT = bo[:, 13:15, :].rearrange("p a b -> p (a b)").rearrange("p (k t) -> p k t", t=256)
                norm_transpose(lambda i: mem[si, i * 128:(i + 1) * 128, :], 2, mem_norm_g, mhT, [[BO(13), BO(14)]] * 2, 256)
                mkT = Db[:, 0, :].rearrange("p (k t) -> p k t", t=256)
                mv = Db[:, 1, :].rearrange("p (m c) -> p m c", c=1024)
                for mh in range(4):
                    banks = []
                    for e in range(2):
                        wv, wk = wget(("kv", mh * 2 + e))
                        b_ = ps_alloc()
                        if not P.dry:
                            for kc in range(NKC):
                                mm(psb[b_][:, 0:256], wv[:, kc, :], mhT[:, kc, :], kc == 0, kc == NKC - 1,
                                   [wk, BO(13), BO(14)], [PSB(b_)])
                        banks.append(b_)
                    if not P.dry:
                        sqv = TB(0)
                        for e in range(2):
                            act(sqv[:, e * 256:(e + 1) * 256], psb[banks[e]][:, 0:256], AF.Square, [],
                                [PSB(banks[e]), TK(0)])
                        sb_ = ps_alloc()
                        for e in range(2):
                            mm(psb[sb_][:, 0:256], ones[:], sqv[:, e * 256:(e + 1) * 256], e == 0, e == 1,
                               ["ones", TK(0)], [PSB(sb_)])
                        act(T(1)[:, 0:256], psb[sb_][:, 0:256], AF.Ln, [], [PSB(sb_), TK(1)], scale=1.0 / 256, bias=EPS)
                        ps_free(sb_)
                        act(T(1)[:, 0:256], T(1)[:, 0:256], AF.Exp, [], [TK(1)], scale=-0.5)
                        for e in range(2):
                            stt(mkT[:, mh * 2 + e, :], psb[banks[e]][:, 0:256], gmk[:, e:e + 1], T(1)[:, 0:256],
                                ALU.mult, ALU.mult, [TK(1), "gmk"], [PSB(banks[e])] + MK)
                    for b_ in banks:
                        ps_free(b_)
                for nbg in range(2):
                    vb = [ps_alloc(), ps_alloc()]
                    for q in range(4):
                        wv, wk = wget(("kv", 8 + nbg * 4 + q))
                        if P.dry:
                            continue
                        for mt in range(2):
                            for kc in range(NKC):
                                mm(psb[vb[mt]][:, q * 128:(q + 1) * 128], mhT[:, kc, mt * 128:(mt + 1) * 128],
                                   wv[:, kc, :], q == 0 and kc == 0, q == 3 and kc == NKC - 1,
                                   [wk, BO(13), BO(14)], [PSB(vb[mt])])
                    for mt in range(2):
                        if not P.dry:
                            cp("act" if mt == 0 else "dve", mv[:, mt, nbg * 512:(nbg + 1) * 512], psb[vb[mt]][:],
                               [], [PSB(vb[mt])] + MV)
                        ps_free(vb[mt])

                for h in range(4):
                    qT = [bo[:, 4 + g, :] for g in range(3)]
                    kT = [bo[:, 7 + g, :] for g in range(3)]
                    Vt = [bo[:, 10 + g, :].rearrange("p (j e) -> p j e", e=128) for g in range(3)]
                    szc = bo[:, 13, :]
                    for g in range(3):
                        for which, cbase, dst, gn, gkey in ((0, C_Q, qT[g], gq, "gq"), (1, C_K, kT[g], gk, "gk")):
                            wv, wk = wget(("in", (cbase + g * 512 + h * 128) // 128))
                            if P.dry:
                                continue
                            for c in range(4):
                                qk_chunk(wv, wk, c, gn[:, g:g + 1], dst[:, c * 512:(c + 1) * 512],
                                         [BO((4 if which == 0 else 7) + g)])
                        wv, wk = wget(("in", (C_V + g * 512 + h * 128) // 128))
                        if P.dry:
                            continue
                        d = DIL[g]
                        nbg_ = 16 // d
                        for jg in range(4):
                            b_ = ps_alloc()
                            for jj in range(4):
                                j = jg * 4 + jj
                                r, blk = j // nbg_, j % nbg_
                                st = blk * 128 * d + r
                                for kc in range(NKC):
                                    mm(psb[b_][:, jj * 128:(jj + 1) * 128], hT[:, kc, st:st + 127 * d + 1:d],
                                       wv[:, kc, :], jj == 0 and kc == 0, jj == 3 and kc == NKC - 1,
                                       [wk, "hT.0", "hT.1"], [PSB(b_)])
                            flush_pending()
                            cp("act" if jg % 2 == 0 else "dve", Vt[g][:, jg * 4:(jg + 1) * 4, :],
                               psb[b_][:].rearrange("p (j e) -> p j e", e=128), [], [PSB(b_), BO(10 + g)])
                            ps_free(b_)
                    wv, wk = wget(("in", (C_ZA + h * 128) // 128))
                    if P.dry:
                        continue
                    for c in range(4):
                        b_ = proj_fm(wv, wk, c)
                        flush_pending()
                        act(szc[:, c * 512:(c + 1) * 512], psb[b_][:], AF.Silu, [], [PSB(b_), BO(13)])
                        ps_free(b_)

                    alljobs = []
                    for c in range(4):
                        jobs = []
                        for p_ in range(2):
                            tiles = []
                            for qi in range(2):
                                b = 4 * c + 2 * p_ + qi
                                oc = slice((b - 4 * c) * 128, (b - 4 * c + 1) * 128)
                                qa = qT[0][:, b * 128:(b + 1) * 128]
                                if b >= 1:
                                    tiles.append((2 * qi, kT[0][:, (b - 1) * 128:b * 128], qa, Vt[0][:, b - 1, :], oc, 0))
                                tiles.append((2 * qi + 1, kT[0][:, b * 128:(b + 1) * 128], qa, Vt[0][:, b, :], oc, 0))
                            jobs.append((tiles, m01[:], "m01", 128))
                        for p_ in range(2):
                            tiles = []
                            for qi in range(2):
                                r = 2 * p_ + qi
                                oc = slice(r, 512, 4)
                                qa = qT[1][:, c * 512 + r:(c + 1) * 512:4]
                                if c >= 1:
                                    kb = c - 1
                                    tiles.append((2 * qi, kT[1][:, kb * 512 + r:(kb + 1) * 512:4], qa,
                                                  Vt[1][:, r * 4 + kb, :], oc, 1))
                                tiles.append((2 * qi + 1, kT[1][:, c * 512 + r:(c + 1) * 512:4], qa,
                                              Vt[1][:, r * 4 + c, :], oc, 1))
                            jobs.append((tiles, m01[:], "m01", 128))
                        tiles = []
                        for r in range(16):
                            tiles.append((r, kT[2][:, r:2048:16], qT[2][:, c * 512 + r:(c + 1) * 512:16],
                                          Vt[2][:, r, :], slice(r, 512, 16), 2))
                        jobs.append((tiles, m2[:, c, :], "m2", 32))
                        for ji, jb in enumerate(jobs):
                            alljobs.append((c, ji, len(jobs), jb))

                    cstate = {}
                    PSLOTS = (2, 3, 0)

                    def job_front(idx):
                        c, ji, nj, (tiles, mask, mkey, w_) = alljobs[idx]
                        if ji == 0:
                            cstate[c] = {"numb": ps_alloc(), "denb": ps_alloc(), "done": 0,
                                         "nmm": sum(len(j[3][0]) for j in alljobs if j[0] == c)}
                        sb_ = ps_alloc()
                        pslot = PSLOTS[idx % 3]
                        for ti, (slot, ka, qa, va, oc, g_) in enumerate(tiles):
                            mm(psb[sb_][:, slot * w_:(slot + 1) * w_], ka, qa, ti == 0, ti == len(tiles) - 1,
                               [BO(4 + g_), BO(7 + g_)], [PSB(sb_)])
                        act(TB(pslot)[:, 0:512], psb[sb_][:], AF.Exp, [], [PSB(sb_), TK(pslot)], scale=128 ** -0.5)
                        ps_free(sb_)
                        tt("dve", TB(pslot)[:, 0:512], TB(pslot)[:, 0:512], mask, ALU.mult, [mkey], [TK(pslot)])

                    def job_back(idx):
                        c, ji, nj, (tiles, mask, mkey, w_) = alljobs[idx]
                        st_ = cstate[c]
                        numb, denb = st_["numb"], st_["denb"]
                        pslot = PSLOTS[idx % 3]
                        pv_ = TB(pslot)
                        for (slot, ka, qa, va, oc, g_) in tiles:
                            first = st_["done"] == 0
                            last = st_["done"] == st_["nmm"] - 1
                            rhs = pv_[:, slot * w_:(slot + 1) * w_]
                            mm(psb[numb][:, oc], va, rhs, first, last, [TK(pslot), BO(10 + g_)], [PSB(numb)])
                            mm(psb[denb][:, oc], ones[:], rhs, first, last, [TK(pslot), "ones"], [PSB(denb)])
                            st_["done"] += 1
                        if ji == nj - 1:
                            P.op("dve", lambda e, denb=denb: e.reciprocal(out=T(4), in_=psb[denb][:]), [],
                                 [PSB(denb), TK(4)])
                            ps_free(denb)
                            tt("dve", T(4), T(4), szc[:, c * 512:(c + 1) * 512], ALU.mult, [BO(13)], [TK(4)])
                            tt("dve", bo[:, h, c * 512:(c + 1) * 512], psb[numb][:], T(4), ALU.mult, [TK(4)],
                               [PSB(numb), BO(h)])
                            ps_free(numb)

                    flush_pending()
                    NJ = len(alljobs)
                    for idx in range(NJ + 2):
                        if idx < NJ:
                            job_front(idx)
                        if idx >= 2:
                            job_back(idx - 2)

                uf = bo[:, 12:15, :].rearrange("p a b -> p (a b)").bitcast(F32)
                ukeys = [BO(12), BO(13), BO(14)]
                if not P.dry:
                    P.op("pool", lambda e: e.memset(uf[:, 0:2], 0.0), [], ukeys)
                for j in range(8):
                    (wB, kB), (wC, kC), (wV, kV), (wZ, kZ) = wget_group(
                        [[("in", (cb_ + j * 128) // 128)] for cb_ in (C_CB, C_CC, C_CV, C_CZ)])
                    if P.dry:
                        continue
                    for c in range(4):
                        bC = proj_fm(wC, kC, c)
                        bV = proj_fm(wV, kV, c)
                        bB = proj_fm(wB, kB, c)
                        bZ = proj_fm(wZ, kZ, c)
                        cs = slice(c * 512, (c + 1) * 512)
                        cp("act", T(0), psb[bC][:], [], [PSB(bC), TK(0)])
                        ps_free(bC)
                        tt("dve", uf[:, 2 + c * 512:2 + (c + 1) * 512], psb[bV][:], T(0), ALU.mult, [TK(0)],
                           [PSB(bV)] + ukeys)
                        ps_free(bV)
                        ts("pool", T(1), uf[:, 2 + c * 512:2 + (c + 1) * 512], cw[:, 0, j:j + 1], ALU.mult,
                           ukeys + ["cw"], [TK(1)])
                        stt(T(1), uf[:, 1 + c * 512:1 + (c + 1) * 512], cw[:, 1, j:j + 1], T(1), ALU.mult, ALU.add,
                            ukeys + ["cw"], [TK(1)])
                        stt(T(1), uf[:, c * 512:(c + 1) * 512], cw[:, 2, j:j + 1], T(1), ALU.mult, ALU.add,
                            ukeys + ["cw"], [TK(1)])
                        act(T(2), psb[bZ][:], AF.Silu, [], [PSB(bZ), TK(2)])
                        ps_free(bZ)
                        tt("dve", T(2), psb[bB][:], T(2), ALU.mult, [], [PSB(bB), TK(2)])
                        ps_free(bB)
                        tt("pool", bo[:, 4 + j, cs], T(1), T(2), ALU.mult, [TK(1), TK(2)], [BO(4 + j)])

                for mh in range(4):
                    wqz = wget_group([[("in", (cb_ + mh * 256 + e * 128) // 128)] for cb_ in (C_MQ, C_MZ)
                                      for e in range(2)])
                    wq, wz = wqz[0:2], wqz[2:4]
                    if P.dry:
                        continue
                    mq = TB(2)

                    def mem_front1(c):
                        t_sq = 0 if c % 2 == 0 else 5
                        qb = []
                        for e in range(2):
                            qb.append(proj_fm(wq[e][0], wq[e][1], c))
                            sqv = TB(t_sq)
                            act(sqv[:, e * 512:(e + 1) * 512], psb[qb[e]][:], AF.Square, [], [PSB(qb[e]), TK(t_sq)])
                        sb_ = rms_b1(qb, t_sq)
                        return qb, sb_

                    def mem_front2(c, qb, sb_):
                        pslot = 0 if c % 2 == 0 else 5
                        rms_b2(qb, sb_, [gmq[:, 0:1], gmq[:, 1:2]], [mq[:, 0:512], mq[:, 512:1024]],
                               [[TK(2)], [TK(2)]], 256, 1)
                        stb = [ps_alloc(), ps_alloc()]
                        for mt in range(2):
                            for e in range(2):
                                mm(psb[stb[mt]][:], mkT[:, mh * 2 + e, mt * 128:(mt + 1) * 128],
                                   mq[:, e * 512:(e + 1) * 512], e == 0, e == 1, MK + [TK(2)], [PSB(stb[mt])])
                        pT = TB(pslot)
                        for mt in range(2):
                            act(pT[:, mt * 512:(mt + 1) * 512], psb[stb[mt]][:], AF.Exp, [],
                                [PSB(stb[mt]), TK(pslot)], scale=256 ** -0.5)
                            ps_free(stb[mt])

                    def mem_back(c):
                        cs = slice(c * 512, (c + 1) * 512)
                        pslot = 0 if c % 2 == 0 else 5
                        pT = TB(pslot)
                        mob = [ps_alloc(), ps_alloc()]
                        dnb = ps_alloc()
                        for mt in range(2):
                            mm(psb[dnb][:], ones[:], pT[:, mt * 512:(mt + 1) * 512], mt == 0, mt == 1,
                               ["ones", TK(pslot)], [PSB(dnb)])
                        for e in range(2):
                            for mt in range(2):
                                mm(psb[mob[e]][:], mv[:, mt, mh * 256 + e * 128:mh * 256 + (e + 1) * 128],
                                   pT[:, mt * 512:(mt + 1) * 512], mt == 0, mt == 1, MV + [TK(pslot)], [PSB(mob[e])])
                        P.op("dve", lambda e, dnb=dnb: e.reciprocal(out=T(1), in_=psb[dnb][:]), [], [PSB(dnb), TK(1)])
                        ps_free(dnb)
                        for e in range(2):
                            zb = proj_fm(wz[e][0], wz[e][1], c)
                            act(T(3 + e), psb[zb][:], AF.Silu, [], [PSB(zb), TK(3 + e)])
                            ps_free(zb)
                            tt("pool", T(3 + e), T(3 + e), T(1), ALU.mult, [TK(1)], [TK(3 + e)])
                            tt("dve", bo[:, 12 + mh * 2 + e, cs], psb[mob[e]][:], T(3 + e), ALU.mult, [TK(3 + e)],
                               [PSB(mob[e]), BO(12 + mh * 2 + e)])
                            ps_free(mob[e])

                    prev = None
                    for c in range(4):
                        qb, sb_ = mem_front1(c)
                        if prev is not None:
                            mem_back(prev)
                        mem_front2(c, qb, sb_)
                        prev = c
                    mem_back(prev)

                if debug and si == 0 and not P.dry:
                    P.dma("sp", lambda e: e.dma_start(out=dbg[:, 0:16 * 2048], in_=hT[:].rearrange("p a b -> p (a b)")),
                          ch_st, reads=["hT.0", "hT.1"], writes=["dbg0"])
                    P.dma("sp", lambda e: e.dma_start(out=dbg[:, 16 * 2048:36 * 2048],
                                                      in_=bo[:].rearrange("p a b -> p (a b)")),
                          ch_st, reads=[BO(j) for j in range(20)], writes=["dbg1"])

                mT = Db[:].rearrange("p a b -> p (a b)").rearrange("p (k t) -> p k t", t=512)
                for c in range(4):
                    cs = slice(c * 512, (c + 1) * 512)
                    for dc in range(16):
                        gb_ = []
                        for br in range(3):
                            wv, wk = wget(("in", (C_G + br * 2048 + dc * 128) // 128))
                            if not P.dry:
                                gb_.append(proj_fm(wv, wk, c))
                        wA, kA = wget(("ba", dc), ("bc", dc))
                        ab = [ps_alloc(), ps_alloc(), ps_alloc()]
                        if not P.dry:
                            for k in range(4):
                                mm(psb[ab[0]][:], wA[:, k, :], bo[:, k, cs], k == 0, k == 3, [kA, BO(k)], [PSB(ab[0])])
                            for k in range(8):
                                mm(psb[ab[1]][:], wA[:, 4 + k, :], bo[:, 4 + k, cs], k == 0, k == 7, [kA, BO(4 + k)],
                                   [PSB(ab[1])])
                        wM, kM = wget(("bm", dc))
                        if not P.dry:
                            for k in range(8):
                                mm(psb[ab[2]][:], wM[:, k, :], bo[:, 12 + k, cs], k == 0, k == 7, [kM, BO(12 + k)],
                                   [PSB(ab[2])])
                            for br in range(3):
                                act(T(br), psb[gb_[br]][:], AF.Sigmoid, [], [PSB(gb_[br]), TK(br)])
                                ps_free(gb_[br])
                                tt("dve", T(br), psb[ab[br]][:], T(br), ALU.mult, [], [PSB(ab[br]), TK(br)])
                            tt("pool", T(0), T(0), T(1), ALU.add, [TK(1)], [TK(0)])
                            tt("pool", mT[:, dc, :], T(0), T(2), ALU.add, [TK(0), TK(2)], [("mT", dc)])
                        for br in range(3):
                            ps_free(ab[br])
                    for nbk in range(4):
                        cols = slice(nbk * 512, (nbk + 1) * 512)

                        def xload(tq, cols=cols):
                            rows = slice(c * 512 + tq * 128, c * 512 + (tq + 1) * 128)
                            xs_ = 3 + (tq % 3)
                            P.dma("sp", lambda e, xs_=xs_, rows=rows, cols=cols, si=si: e.dma_start(out=T(xs_), in_=x[si, rows, cols]),
                                  ch_x, writes=[TK(xs_)])

                        if not P.dry:
                            for tq in range(3):
                                xload(tq)
                        ob = [ps_alloc() for _ in range(4)]
                        for q in range(4):
                            wv, wk = wget(("out", nbk * 4 + q))
                            if P.dry:
                                continue
                            for tq in range(4):
                                for kc in range(NKC):
                                    mm(psb[ob[tq]][:, q * 128:(q + 1) * 128], mT[:, kc, tq * 128:(tq + 1) * 128],
                                       wv[:, kc, :], kc == 0 and q == 0, kc == NKC - 1 and q == 3,
                                       [wk, ("mT", kc)], [PSB(ob[tq])])
                        for tq in range(4):
                            if not P.dry:
                                rows = slice(c * 512 + tq * 128, c * 512 + (tq + 1) * 128)
                                xs_ = 3 + (tq % 3)
                                tt("dve", T(xs_), psb[ob[tq]][:], T(xs_), ALU.add, [], [PSB(ob[tq]), TK(xs_)])
                                P.dma("sp", lambda e, xs_=xs_, rows=rows, cols=cols, si=si: e.dma_start(out=y[si, rows, cols], in_=T(xs_)),
                                      ch_st, reads=[TK(xs_)], writes=[("y", len(ykeys))])
                                ykeys.append(("y", len(ykeys)))
                                if tq == 0:
                                    xload(3)
                            ps_free(ob[tq])
            if not P.dry:
                P.op("sp", lambda e: e.nop(), reads=ykeys + ["dbg0", "dbg1"], writes=[])
            return wst["req"]

        plan = emit_all(Prog(dry=True), None)
        P = Prog(dry=False)
        emit_all(P, plan)
        P.analyze(eng_sems)
        with nc.Block() as block:
            @block.tensor
            def _(e):
                P.emit("pe", e)

            @block.scalar
            def _(e):
                P.emit("act", e)

            @block.vector
            def _(e):
                P.emit("dve", e)

            @block.gpsimd
            def _(e):
                P.emit("pool", e)

            @block.sync
            def _(e):
                P.emit("sp", e)
    return nc


def _consts():
    bf = ml_dtypes.bfloat16
    k = np.arange(128)[:, None]
    q = np.arange(128)[None, :]
    prev = (k >= q).astype(np.float32)
    cur = (k <= q).astype(np.float32)
    m01 = np.concatenate([prev, cur, prev, cur], axis=1)
    m2 = np.zeros((128, 4, 16, 32), np.float32)
    qi = np.arange(32)[None, :]
    for c in range(4):
        m2[:, c, :, :] = (k <= 32 * c + qi).astype(np.float32)[:, None, :]
    return {
        "c_ident": np.eye(128, dtype=np.float32).astype(bf),
        "c_m01": m01.astype(bf),
        "c_m2": m2.reshape(128, 4 * 512).astype(bf),
    }


_WNAMES = ("norm_g", "mem_norm_g", "w_in", "attn_q_norm", "attn_k_norm", "conv_w", "mem_w_kv",
           "mem_q_norm", "mem_k_norm", "w_br_attn", "w_br_conv", "w_br_mem", "w_out")


def kernel(**inputs):
    ncores = 8
    x = np.ascontiguousarray(np.asarray(inputs["x"], dtype=np.float32))
    mem = np.ascontiguousarray(np.asarray(inputs["mem"], dtype=np.float32))
    B = x.shape[0]
    nseq = B // ncores
    shared = {n: np.ascontiguousarray(np.asarray(inputs[n], dtype=np.float32)) for n in _WNAMES}
    shared.update(_consts())
    nc = build(nseq=nseq)
    in_maps = []
    for i in range(ncores):
        m = dict(shared)
        m["x"] = x[i * nseq:(i + 1) * nseq]
        m["mem"] = mem[i * nseq:(i + 1) * nseq]
        in_maps.append(m)
    res = run_bass_kernel_spmd(nc, in_maps, core_ids=list(range(ncores)))
    return np.concatenate([np.asarray(r["y"], dtype=np.float32) for r in res.results], axis=0)
```
